# Optimizing a Trainium2 kernel written in Bass

```python
import math
import jax, jax.numpy as jnp
from jax import lax
import numpy as np

D_MODEL = 1024
BATCH = 16
SEQ = 2048
DEPTH = 2

HEAD_DIM = 64
N_HEADS_A = D_MODEL // (2 * HEAD_DIM)
N_HEADS_B = D_MODEL // (2 * HEAD_DIM)
N_HEADS_DIFF = D_MODEL // (2 * HEAD_DIM)
DILATED_CONFIGS = ((128, 1), (512, 4), (2048, 16))
WINDOW_Q_BLOCK = 128
MOBA_BLOCK = 256
MOBA_TOPK = 3
MOBA_Q_CHUNK = 32
DIFF_Q_BLOCK = 128
ROPE_THETA = 500000.0
ROPE_DIM = HEAD_DIM // 4
D_FF = -(-(8 * D_MODEL) // (3 * 256)) * 256
NORM_EPS = 1e-5
ATTN_SCALE = HEAD_DIM ** -0.5
NEG_INF = -1e30

kernel_name = 'hybrid_dilated_moba_diffattn_block'


def rmsnorm(x, g):
    xf = x.astype(jnp.float32)
    y = xf * lax.rsqrt(jnp.mean(xf * xf, axis=-1, keepdims=True) + NORM_EPS)
    return (y * g.astype(jnp.float32)).astype(x.dtype)


def rope_tables(positions):
    inv_freq = ROPE_THETA ** (-jnp.arange(0, ROPE_DIM, 2, dtype=jnp.float32) / ROPE_DIM)
    ang = positions.astype(jnp.float32)[..., None] * inv_freq
    return jnp.cos(ang), jnp.sin(ang)


def apply_partial_rope(x, cos, sin):
    half = ROPE_DIM // 2
    xf = x.astype(jnp.float32)
    x1, x2 = xf[..., :half], xf[..., half:ROPE_DIM]
    c, s = cos[:, :, None, :], sin[:, :, None, :]
    out = jnp.concatenate([x1 * c - x2 * s, x2 * c + x1 * s, xf[..., ROPE_DIM:]], axis=-1)
    return out.astype(x.dtype)


def strided_window_attention(q, k, v, n_keys):
    B, L, R, H, dh = q.shape
    bq = math.gcd(L, WINDOW_Q_BLOCK)
    nb = L // bq
    kw_len = bq + n_keys
    pad = ((0, 0), (n_keys, 0), (0, 0), (0, 0), (0, 0))
    idx = jnp.arange(nb)[:, None] * bq + jnp.arange(kw_len)[None, :]
    kw = jnp.take(jnp.pad(k, pad), idx, axis=1)
    vw = jnp.take(jnp.pad(v, pad), idx, axis=1)
    qb = q.reshape(B, nb, bq, R, H, dh)
    logits = jnp.einsum('bnqrhd,bnkrhd->bnrhqk', qb, kw).astype(jnp.float32) * ATTN_SCALE
    rel = jnp.arange(bq)[:, None] + n_keys - jnp.arange(kw_len)[None, :]
    band = (rel >= 0) & (rel <= n_keys)
    valid = idx >= n_keys
    mask = band[None] & valid[:, None, :]
    logits = jnp.where(mask[None, :, None, None], logits, NEG_INF)
    m = jnp.max(logits, axis=-1, keepdims=True)
    p = jnp.exp(logits - m)
    denom = jnp.sum(p, axis=-1, keepdims=True)
    o = jnp.einsum('bnrhqk,bnkrhd->bnqrhd', p / denom, vw.astype(jnp.float32))
    lse = (m + jnp.log(denom))[..., 0]
    o = o.reshape(B, L, R, H, dh)
    lse = lse.transpose(0, 1, 4, 2, 3).reshape(B, L, R, H)
    return o, lse


def dilated_attention(q, k, v):
    B, S, H, dh = q.shape
    outs, lses = [], []
    for window, dilation in DILATED_CONFIGS:
        L = S // dilation
        split = lambda t: t.reshape(B, L, dilation, H, dh)
        o, lse = strided_window_attention(split(q), split(k), split(v), window // dilation)
        outs.append(o.reshape(B, S, H, dh))
        lses.append(lse.reshape(B, S, H))
    w = jax.nn.softmax(jnp.stack(lses, axis=-1), axis=-1)
    out = jnp.einsum('bshg,gbshd->bshd', w, jnp.stack(outs, axis=0))
    return out.astype(q.dtype)


def moba_attention(q, k, v):
    B, S, H, dh = q.shape
    nblk = -(-S // MOBA_BLOCK)
    s_pad = nblk * MOBA_BLOCK
    topk = min(MOBA_TOPK, nblk - 1)
    qh = q.transpose(0, 2, 1, 3)
    pad = ((0, 0), (0, 0), (0, s_pad - S), (0, 0))
    kp = jnp.pad(k.transpose(0, 2, 1, 3), pad)
    vp = jnp.pad(v.transpose(0, 2, 1, 3), pad)
    kb = kp.reshape(B, H, nblk, MOBA_BLOCK, dh)
    vb = vp.reshape(B, H, nblk, MOBA_BLOCK, dh)
    kmean = jnp.mean(kb.astype(jnp.float32), axis=3)
    bi = jnp.arange(B)[:, None, None, None]
    hi = jnp.arange(H)[None, :, None, None]

    def chunk_fn(c):
        start = c * MOBA_Q_CHUNK
        qblk = start // MOBA_BLOCK
        qc = lax.dynamic_slice_in_dim(qh, start, MOBA_Q_CHUNK, axis=2)
        ko = lax.dynamic_slice_in_dim(kp, qblk * MOBA_BLOCK, MOBA_BLOCK, axis=2)
        vo = lax.dynamic_slice_in_dim(vp, qblk * MOBA_BLOCK, MOBA_BLOCK, axis=2)
        qpos = start + jnp.arange(MOBA_Q_CHUNK)
        kpos = qblk * MOBA_BLOCK + jnp.arange(MOBA_BLOCK)
        own = jnp.einsum('bhqd,bhkd->bhqk', qc, ko).astype(jnp.float32) * ATTN_SCALE
        own = jnp.where(kpos[None, :] <= qpos[:, None], own, NEG_INF)
        if topk > 0:
            gate = jnp.einsum('bhqd,bhnd->bhqn', qc.astype(jnp.float32), kmean)
            past = jnp.arange(nblk) < qblk
            gate = jnp.where(past, gate, NEG_INF)
            _, gidx = lax.top_k(gate, topk)
            sel_valid = gidx < qblk
            ks = kb[bi, hi, gidx]
            vs = vb[bi, hi, gidx]
            sel = jnp.einsum('bhqd,bhqnkd->bhqnk', qc, ks).astype(jnp.float32) * ATTN_SCALE
            sel = jnp.where(sel_valid[..., None], sel, NEG_INF)
            logits = jnp.concatenate([sel.reshape(B, H, MOBA_Q_CHUNK, topk * MOBA_BLOCK), own], axis=-1)
            p = jax.nn.softmax(logits, axis=-1)
            p_sel = p[..., :topk * MOBA_BLOCK].reshape(B, H, MOBA_Q_CHUNK, topk, MOBA_BLOCK)
            p_own = p[..., topk * MOBA_BLOCK:]
            o = (jnp.einsum('bhqnk,bhqnkd->bhqd', p_sel, vs.astype(jnp.float32))
                 + jnp.einsum('bhqk,bhkd->bhqd', p_own, vo.astype(jnp.float32)))
        else:
            p_own = jax.nn.softmax(own, axis=-1)
            o = jnp.einsum('bhqk,bhkd->bhqd', p_own, vo.astype(jnp.float32))
        return o

    outs = lax.map(chunk_fn, jnp.arange(S // MOBA_Q_CHUNK))
    out = outs.transpose(1, 0, 3, 2, 4).reshape(B, S, H, dh)
    return out.astype(q.dtype)


def hybrid_ab_mixer(xn, w_in, w_out, cos, sin):
    B, S, _ = xn.shape
    proj = xn @ w_in
    wa, wb = N_HEADS_A * HEAD_DIM, N_HEADS_B * HEAD_DIM
    cuts = np.cumsum([wa, wa, wa, wb, wb])
    qa, ka, va, qb, kb, vb = jnp.split(proj, [int(c) for c in cuts], axis=-1)
    heads = lambda t, h: t.reshape(B, S, h, HEAD_DIM)
    qa = apply_partial_rope(heads(qa, N_HEADS_A), cos, sin)
    ka = apply_partial_rope(heads(ka, N_HEADS_A), cos, sin)
    qb = apply_partial_rope(heads(qb, N_HEADS_B), cos, sin)
    kb = apply_partial_rope(heads(kb, N_HEADS_B), cos, sin)
    oa = dilated_attention(qa, ka, heads(va, N_HEADS_A))
    ob = moba_attention(qb, kb, heads(vb, N_HEADS_B))
    o = jnp.concatenate([oa.reshape(B, S, wa), ob.reshape(B, S, wb)], axis=-1)
    return o @ w_out


def lambda_init_fn(layer_idx):
    return 0.8 - 0.6 * math.exp(-0.3 * layer_idx)


def diff_mixer(xn, w_in, w_out, lq1, lk1, lq2, lk2, subln_g, cos, sin, lambda_init):
    B, S, _ = xn.shape
    H = N_HEADS_DIFF
    wq = 2 * H * HEAD_DIM
    q, k, v = jnp.split(xn @ w_in, [wq, 2 * wq], axis=-1)
    q = apply_partial_rope(q.reshape(B, S, 2 * H, HEAD_DIM), cos, sin)
    k = apply_partial_rope(k.reshape(B, S, 2 * H, HEAD_DIM), cos, sin)
    qh = q.reshape(B, S, H, 2, HEAD_DIM).transpose(0, 2, 3, 1, 4)
    kh = k.reshape(B, S, H, 2, HEAD_DIM).transpose(0, 2, 3, 1, 4)
    vh = v.reshape(B, S, H, 2 * HEAD_DIM).transpose(0, 2, 1, 3)
    lam = (jnp.exp(jnp.sum(lq1.astype(jnp.float32) * lk1.astype(jnp.float32)))
           - jnp.exp(jnp.sum(lq2.astype(jnp.float32) * lk2.astype(jnp.float32)))
           + lambda_init)
    kpos = jnp.arange(S)

    def block_fn(c):
        start = c * DIFF_Q_BLOCK
        qc = lax.dynamic_slice_in_dim(qh, start, DIFF_Q_BLOCK, axis=3)
        logits = jnp.einsum('bhiqd,bhikd->bhiqk', qc, kh).astype(jnp.float32) * ATTN_SCALE
        qpos = start + jnp.arange(DIFF_Q_BLOCK)
        logits = jnp.where(kpos[None, :] <= qpos[:, None], logits, NEG_INF)
        p = jax.nn.softmax(logits, axis=-1)
        attn = p[:, :, 0] - lam * p[:, :, 1]
        return jnp.einsum('bhqk,bhkd->bhqd', attn, vh.astype(jnp.float32))

    outs = lax.map(block_fn, jnp.arange(S // DIFF_Q_BLOCK))
    o = outs.transpose(1, 0, 3, 2, 4).reshape(B, S, H, 2 * HEAD_DIM).astype(xn.dtype)
    o = rmsnorm(o, subln_g) * (1.0 - lambda_init)
    return o.reshape(B, S, H * 2 * HEAD_DIM) @ w_out


def swiglu(xn, w_gate, w_up, w_down):
    return (jax.nn.silu(xn @ w_gate) * (xn @ w_up)) @ w_down


def setup_inputs(seed: int = 0) -> dict:
    key = jax.random.key(seed)
    ks = jax.random.split(key, 20)
    n_even = (DEPTH + 1) // 2
    n_odd = DEPTH // 2
    w_ab_in = 3 * (N_HEADS_A + N_HEADS_B) * HEAD_DIM
    w_ab_out = (N_HEADS_A + N_HEADS_B) * HEAD_DIM
    w_diff_in = 3 * N_HEADS_DIFF * 2 * HEAD_DIM
    w_diff_out = N_HEADS_DIFF * 2 * HEAD_DIM
    nrm = lambda k, shape, scale: scale * jax.random.normal(k, shape, jnp.float32)
    gain = lambda k, shape: 1.0 + nrm(k, shape, 0.02)
    offset = jax.random.randint(ks[1], (BATCH, 1), 0, 4096, dtype=jnp.int32)
    positions = (offset + jnp.arange(SEQ, dtype=jnp.int32)[None, :]).astype(jnp.int32)
    return {
        'x': nrm(ks[0], (BATCH, SEQ, D_MODEL), 1.0),
        'positions': positions,
        'ab_norm_g': gain(ks[2], (n_even, D_MODEL)),
        'ab_w_in': nrm(ks[3], (n_even, D_MODEL, w_ab_in), D_MODEL ** -0.5),
        'ab_w_out': nrm(ks[4], (n_even, w_ab_out, D_MODEL), w_ab_out ** -0.5),
        'diff_norm_g': gain(ks[5], (n_odd, D_MODEL)),
        'diff_w_in': nrm(ks[6], (n_odd, D_MODEL, w_diff_in), D_MODEL ** -0.5),
        'diff_w_out': nrm(ks[7], (n_odd, w_diff_out, D_MODEL), w_diff_out ** -0.5),
        'diff_lambda_q1': nrm(ks[8], (n_odd, HEAD_DIM), 0.1),
        'diff_lambda_k1': nrm(ks[9], (n_odd, HEAD_DIM), 0.1),
        'diff_lambda_q2': nrm(ks[10], (n_odd, HEAD_DIM), 0.1),
        'diff_lambda_k2': nrm(ks[11], (n_odd, HEAD_DIM), 0.1),
        'diff_subln_g': gain(ks[12], (n_odd, 2 * HEAD_DIM)),
        'ffn_norm_g': gain(ks[13], (DEPTH, D_MODEL)),
        'ffn_w_gate': nrm(ks[14], (DEPTH, D_MODEL, D_FF), D_MODEL ** -0.5),
        'ffn_w_up': nrm(ks[15], (DEPTH, D_MODEL, D_FF), D_MODEL ** -0.5),
        'ffn_w_down': nrm(ks[16], (DEPTH, D_FF, D_MODEL), D_FF ** -0.5),
        'final_norm_g': gain(ks[17], (D_MODEL,)),
    }


def reference(x, positions, ab_norm_g, ab_w_in, ab_w_out, diff_norm_g, diff_w_in, diff_w_out,
              diff_lambda_q1, diff_lambda_k1, diff_lambda_q2, diff_lambda_k2, diff_subln_g,
              ffn_norm_g, ffn_w_gate, ffn_w_up, ffn_w_down, final_norm_g):
    cos, sin = rope_tables(positions)
    h = x
    for layer in range(DEPTH):
        i = layer // 2
        if layer % 2 == 0:
            h = h + hybrid_ab_mixer(rmsnorm(h, ab_norm_g[i]), ab_w_in[i], ab_w_out[i], cos, sin)
        else:
            h = h + diff_mixer(rmsnorm(h, diff_norm_g[i]), diff_w_in[i], diff_w_out[i],
                               diff_lambda_q1[i], diff_lambda_k1[i], diff_lambda_q2[i],
                               diff_lambda_k2[i], diff_subln_g[i], cos, sin,
                               lambda_init_fn(layer))
        h = h + swiglu(rmsnorm(h, ffn_norm_g[layer]), ffn_w_gate[layer], ffn_w_up[layer], ffn_w_down[layer])
    return rmsnorm(h, final_norm_g)
```

```python
import math
import os
import itertools
from contextlib import ExitStack

import numpy as np
import ml_dtypes

import concourse.bass as bass
import concourse.mybir as mybir
from concourse.bass_utils import run_bass_kernel_spmd

F32 = mybir.dt.float32
BF16 = mybir.dt.bfloat16
I32 = mybir.dt.int32
ALU = mybir.AluOpType
AF = mybir.ActivationFunctionType
AX = mybir.AxisListType

S = 2048
D = 1024
DFF = 2816
NJ = DFF // 128
EPS = 1e-5
BIG = 30000.0
LAMBDA_INIT = 0.8 - 0.6 * math.exp(-0.3 * 1)
TWO_PI = 2.0 * math.pi
CW1 = 6.28125
CW2 = TWO_PI - CW1
MAGIC = 12582912.0
PI_SAFE = 3.1415925
TMW = 2432
NSLOT = 10
BLK = 256


_reg_cache = {}


def _region(ap):
    key = (ap.tensor.name, ap.offset, tuple(ap.ap), str(ap.dtype))
    r = _reg_cache.get(key)
    if r is not None:
        return r
    es = mybir.dt.size(ap.dtype)
    T = 1
    for s_ in ap.tensor.shape[1:]:
        T *= s_
    col = ap.offset % T
    free = sorted([(st, n) for st, n in list(ap.ap)[1:] if n > 1 and st != 0])
    run = 1
    if free and free[0][0] == 1:
        run = free[0][1]
        free = free[1:]
    while free and free[0][0] == run:
        run *= free[0][1]
        free = free[1:]
    cnt = 1
    for _, n in free:
        cnt *= n
    ivs = []
    if cnt <= 64:
        for combo in itertools.product(*[range(n) for _, n in free]):
            st = col + sum(c * f[0] for c, f in zip(combo, free))
            ivs.append((st, st + run))
    else:
        hi = col + sum((n - 1) * st for st, n in free) + run
        ivs.append((col, hi))
    blocks = set()
    name = ap.tensor.name
    for lo, hi in ivs:
        for b in range((lo * es) // BLK, (hi * es - 1) // BLK + 1):
            blocks.add((name, b))
    r = tuple(blocks)
    _reg_cache[key] = r
    return r


class Op:
    __slots__ = ("eng", "fn", "rd", "wr", "dma", "deps", "sig", "sigidx", "waits", "dsem", "dval", "idx", "has_dep")


class Rec:
    ENGS = ("pe", "act", "dve", "pool", "sp")

    def __init__(self):
        self.ops = []
        self.last_writer = {}
        self.readers = {}

    def add(self, eng, fn, reads=(), writes=(), dma=False, extra_deps=()):
        op = Op()
        op.eng = eng
        op.fn = fn
        op.dma = dma
        op.idx = len(self.ops)
        op.has_dep = False
        deps = set(extra_deps)
        rd = set()
        wr = set()
        for ap in reads:
            if ap is not None:
                rd.update(_region(ap))
        for ap in writes:
            if ap is not None:
                wr.update(_region(ap))
        lw = self.last_writer
        rdrs = self.readers
        for b in rd:
            w = lw.get(b)
            if w is not None:
                deps.add(w)
        for b in wr:
            w = lw.get(b)
            if w is not None:
                deps.add(w)
            rr = rdrs.get(b)
            if rr:
                deps.update(rr.values())
        for b in wr:
            lw[b] = op.idx
            rdrs[b] = {}
        rkey = eng if not dma else ("dma", op.idx)
        for b in rd:
            if b in wr:
                continue
            d_ = rdrs.get(b)
            if d_ is None:
                d_ = rdrs[b] = {}
            d_[rkey] = op.idx
        deps.discard(op.idx)
        op.deps = deps
        self.ops.append(op)
        return op.idx

    def finalize(self, nc, stack):
        ops = self.ops
        for op in ops:
            keep = set()
            for d_ in op.deps:
                p = ops[d_]
                if p.eng == "pe" and op.eng == "pe" and not p.dma and not op.dma:
                    continue
                keep.add(d_)
                p.has_dep = True
            op.deps = keep
        engsem = {e: stack.enter_context(nc.semaphore("sem_" + e)) for e in self.ENGS}
        KD = 8
        dmasem = {e: [stack.enter_context(nc.semaphore("dsem_%s%d" % (e, i))) for i in range(KD)] for e in ("pool", "sp")}
        cnt = {e: 0 for e in self.ENGS}
        dcnt = {"pool": 0, "sp": 0}
        for op in ops:
            if op.dma:
                j = dcnt[op.eng]
                dcnt[op.eng] += 1
                op.dsem = dmasem[op.eng][j % KD]
                op.dval = 16 * (j // KD + 1)
                op.sig = False
            else:
                op.sig = op.has_dep and op.fn is not None
                if op.sig:
                    cnt[op.eng] += 1
                    op.sigidx = cnt[op.eng]
        maxw = {e: {} for e in self.ENGS}
        for op in ops:
            need = {}
            if op.dma and op.dval > 16:
                need[op.dsem] = op.dval - 16
            for d_ in op.deps:
                p = ops[d_]
                if p.dma:
                    sem, val = p.dsem, p.dval
                else:
                    sem, val = engsem[p.eng], p.sigidx
                if need.get(sem, 0) < val:
                    need[sem] = val
            mw = maxw[op.eng]
            waits = []
            for sem, val in need.items():
                if mw.get(sem, 0) < val:
                    mw[sem] = val
                    waits.append((sem, val))
            op.waits = waits
        self.engsem = engsem

    def emit(self, eng, e):
        sem_e = self.engsem[eng]
        for op in self.ops:
            if op.eng != eng:
                continue
            for sem, val in op.waits:
                e.wait_ge(sem, val)
            if op.fn is None:
                continue
            ins = op.fn(e)
            if op.dma:
                ins.then_inc(op.dsem, 16)
            elif op.sig:
                ins.then_inc(sem_e, 1)


class Prog:
    def __init__(self, nc, stack, stop_after=None):
        self.nc = nc
        self.R = Rec()
        self.stop_after = stop_after
        st = stack
        d = lambda name, shape, dt, kind="ExternalInput": nc.dram_tensor(name, shape, dt, kind=kind).ap()
        self.x = d("x", [2, S, D], F32)
        self.pos = d("positions", [2, S], I32)
        self.ab_norm_g = d("ab_norm_g", [1, D], F32)
        self.ab_w_in = d("ab_w_in", [1, D, 3072], F32)
        self.ab_w_out = d("ab_w_out", [1, D, D], F32)
        self.diff_norm_g = d("diff_norm_g", [1, D], F32)
        self.diff_w_in = d("diff_w_in", [1, D, 3072], F32)
        self.diff_w_out = d("diff_w_out", [1, D, D], F32)
        self.lq1 = d("diff_lambda_q1", [1, 64], F32)
        self.lk1 = d("diff_lambda_k1", [1, 64], F32)
        self.lq2 = d("diff_lambda_q2", [1, 64], F32)
        self.lk2 = d("diff_lambda_k2", [1, 64], F32)
        self.subln_g = d("diff_subln_g", [1, 128], F32)
        self.ffn_norm_g = d("ffn_norm_g", [2, D], F32)
        self.w_gate = d("ffn_w_gate", [2, D, DFF], F32)
        self.w_up = d("ffn_w_up", [2, D, DFF], F32)
        self.w_down = d("ffn_w_down", [2, DFF, D], F32)
        self.final_g = d("final_norm_g", [1, D], F32)
        self.c_ident = d("c_ident", [128, 128], BF16)
        self.c_perm = d("c_perm", [128, 128], BF16)
        self.c_tri = d("c_tri", [128, 128], BF16)
        self.c_tm = d("c_tm", [128, TMW], BF16)
        self.c_invf = d("c_invf", [128, 1], F32)
        self.y = d("y", [2, S, D], F32, kind="ExternalOutput")

        sb = lambda name, shape, dt: st.enter_context(nc.sbuf_tensor(name, shape, dt))
        self.H = sb("H", [128, 16, D], F32)
        self.XNT = sb("XNT", [128, 8, S], BF16)
        self.AR = sb("AR", [128, 32768], BF16)
        self.WB = sb("WB", [128, NSLOT, 1024], BF16)
        self.CT = sb("CT", [128, S], BF16)
        self.ST = sb("ST", [128, S], BF16)
        self.PT = sb("PT", [128, 3, 512], BF16)
        self.RAW = sb("RAW", [128, 2, 512], BF16)
        self.TMP = sb("TMP", [128, 2, 512], F32)
        self.SCR = sb("SCR", [128, 2048], F32)
        self.IDENT = sb("IDENT", [128, 128], BF16)
        self.PERM = sb("PERM", [128, 128], BF16)
        self.TRI = sb("TRI", [128, 128], BF16)
        self.ONES = sb("ONES", [128, 128], BF16)
        self.INVF = sb("INVF", [128, 1], F32)
        self.SSQ = sb("SSQ", [128, 16], F32)
        self.RSTD = sb("RSTD", [128, 16], F32)
        self.KM = sb("KM", [128, 8], F32)
        self.KMH = sb("KMH", [128, 8], BF16)
        self.KML = sb("KML", [128, 8], BF16)
        self.GM = sb("GM", [128, 16], F32)
        self.T8 = sb("T8", [128, 32], F32)
        self.SELB = sb("SELB", [128, 128], BF16)
        self.LAM = sb("LAM", [128, 4, 64], F32)
        self.LSC = sb("LSC", [128, 8], F32)
        self.GS = sb("GS", [128, 1], F32)
        self.banks = [st.enter_context(nc.psum_tensor("PS%d" % i, [128, 512], F32)) for i in range(8)]

        self.OT = self.AR[:, 0:16384].rearrange("p (k t) -> p k t", k=8)
        self.HID = self.AR[:, 0:NJ * 1024].rearrange("p (j t) -> p j t", j=NJ)
        self.scr_bf = self.SCR[:].bitcast(BF16)
        self.GBC = self.SCR[:, 0:1024]
        self.OUTT = self.SCR[:, 1024:2048]
        self.JUNK = self.scr_bf[:, 2048:3072]
        self.TM = self.scr_bf[:, 0:TMW]
        self.SELBT = self.scr_bf[:, 0:2048]
        self.xn_tmp = self.TMP[:].rearrange("p a b -> p (a b)").bitcast(BF16)
        self.pt_i = 0
        self.slot_i = 0

    def mm(self, out, lhsT, rhs, start=True, stop=True):
        self.R.add("pe", lambda e: e.matmul(out, lhsT, rhs, start=start, stop=stop), reads=(lhsT, rhs), writes=(out,))

    def tr(self, out, in_, ident):
        self.R.add("pe", lambda e: e.transpose(out, in_, ident), reads=(in_, ident), writes=(out,))

    def act(self, out, in_, func, bias=None, scale=None, accum_out=None):
        kw = {}
        rd = [in_]
        if bias is not None:
            kw["bias"] = bias
            if not isinstance(bias, (int, float)):
                rd.append(bias)
        if scale is not None:
            kw["scale"] = scale
            if not isinstance(scale, (int, float)):
                rd.append(scale)
        wr = [out]
        if accum_out is not None:
            kw["accum_out"] = accum_out
            wr.append(accum_out)
        self.R.add("act", lambda e: e.activation(out, in_, func, **kw), reads=rd, writes=wr)

    def tt(self, out, in0, in1, op, eng="dve"):
        self.R.add(eng, lambda e: e.tensor_tensor(out, in0, in1, op), reads=(in0, in1), writes=(out,))

    def ts(self, out, in0, s1, s2, op0, op1=None, eng="dve"):
        rd = [in0]
        for s_ in (s1, s2):
            if s_ is not None and not isinstance(s_, (int, float)):
                rd.append(s_)
        if op1 is None:
            self.R.add(eng, lambda e: e.tensor_scalar(out, in0, s1, None, op0), reads=rd, writes=(out,))
        else:
            self.R.add(eng, lambda e: e.tensor_scalar(out, in0, s1, s2, op0, op1), reads=rd, writes=(out,))

    def stt(self, out, in0, scalar, in1, op0, op1, eng="dve"):
        rd = [in0, in1]
        if not isinstance(scalar, (int, float)):
            rd.append(scalar)
        self.R.add(eng, lambda e: e.scalar_tensor_tensor(out, in0, scalar, in1, op0, op1), reads=rd, writes=(out,))

    def cp(self, out, in_, eng="dve"):
        if eng == "act":
            self.R.add("act", lambda e: e.copy(out, in_), reads=(in_,), writes=(out,))
        else:
            self.R.add(eng, lambda e: e.tensor_copy(out, in_), reads=(in_,), writes=(out,))

    def memset(self, out, val, eng="dve"):
        self.R.add(eng, lambda e: e.memset(out, val), writes=(out,))

    def recip(self, out, in_, fast=False):
        if fast:
            self.act(out, in_, AF.Ln)
            self.act(out, out, AF.Exp, scale=-1.0)
        else:
            self.R.add("dve", lambda e: e.reciprocal(out, in_), reads=(in_,), writes=(out,))

    def red(self, out, in_, op):
        self.R.add("dve", lambda e: e.tensor_reduce(out, in_, AX.X, op), reads=(in_,), writes=(out,))

    def max8(self, out, in_):
        self.R.add("dve", lambda e: e.max(out, in_), reads=(in_,), writes=(out,))

    def dma(self, out, in_, q="pool", sb_out=True, sb_in=False):
        return self.R.add(q, lambda e: e.dma_start(out=out, in_=in_), reads=((in_,) if sb_in else ()),
                          writes=((out,) if sb_out else ()), dma=True)

    def next_pt(self):
        i = self.pt_i
        self.pt_i = (i + 1) % 3
        return self.PT[:, i, :]

    def wslot(self, s_):
        return self.WB[:, s_, :].rearrange("p (k n) -> p k n", k=8)

    def load_consts(self):
        self.dma(self.IDENT[:], self.c_ident, q="sp")
        self.dma(self.PERM[:], self.c_perm, q="sp")
        self.dma(self.TRI[:], self.c_tri, q="sp")
        self.dma(self.INVF[:], self.c_invf, q="sp")
        self.memset(self.ONES[:], 1.0)
        for i, v in enumerate((self.lq1, self.lk1, self.lq2, self.lk2)):
            self.dma(self.LAM[:, i, :], v.partition_broadcast(128), q="sp")
        self.dma(self.GS[:], self.subln_g.rearrange("o d -> d o"), q="sp")
        L = self.LSC
        self.tt(self.LAM[:, 0, :], self.LAM[:, 0, :], self.LAM[:, 1, :], ALU.mult)
        self.tt(self.LAM[:, 2, :], self.LAM[:, 2, :], self.LAM[:, 3, :], ALU.mult)
        self.red(L[:, 0:1], self.LAM[:, 0, :], ALU.add)
        self.red(L[:, 1:2], self.LAM[:, 2, :], ALU.add)
        self.act(L[:, 2:4], L[:, 0:2], AF.Exp)
        self.tt(L[:, 4:5], L[:, 3:4], L[:, 2:3], ALU.subtract)
        self.ts(L[:, 5:6], L[:, 4:5], -LAMBDA_INIT, None, ALU.add)
        self.ts(self.GS[:], self.GS[:], 1.0 - LAMBDA_INIT, None, ALU.mult)
        self.NEGLAM = L[:, 5:6]

    def rope_tables(self, s_):
        arf = self.AR[:].bitcast(F32)
        posi = self.AR[:].bitcast(I32)[:, 0:2048]
        ang = arf[:, 2048:4096]
        t_ = arf[:, 4096:6144]
        r_ = arf[:, 6144:8192]
        self.dma(posi, self.pos[s_:s_ + 1, :].partition_broadcast(128), q="sp")
        posf = arf[:, 8192:10240]
        self.cp(posf, posi)
        self.ts(ang, posf, self.INVF[:, 0:1], None, ALU.mult)
        self.ts(t_, ang, 1.0 / TWO_PI, MAGIC, ALU.mult, ALU.add)
        self.ts(t_, t_, -MAGIC, None, ALU.add)
        self.stt(r_, t_, -CW1, ang, ALU.mult, ALU.add)
        self.stt(r_, t_, -CW2, r_, ALU.mult, ALU.add)
        self.ts(r_, r_, -PI_SAFE, PI_SAFE, ALU.max, ALU.min)
        self.act(self.ST[:], r_, AF.Sin)
        self.stt(t_, r_, -1.0, r_, ALU.mult, ALU.max)
        self.ts(t_, t_, -1.0, math.pi / 2, ALU.mult, ALU.add)
        self.act(self.CT[:], t_, AF.Sin)

    def norm_to_xnt(self, gain_ap):
        H = self.H
        self.dma(self.GBC, gain_ap.partition_broadcast(128), q="sp")
        for c in range(16):
            self.act(self.JUNK, H[:, c, :], AF.Square, accum_out=self.SSQ[:, c:c + 1])
        self.ts(self.RSTD[:], self.SSQ[:], 1.0 / D, EPS, ALU.mult, ALU.add)
        self.act(self.RSTD[:], self.RSTD[:], AF.Sqrt)
        self.recip(self.RSTD[:], self.RSTD[:])
        for c in range(16):
            xn = self.xn_tmp[:, (c % 2) * 1024:(c % 2 + 1) * 1024]
            self.stt(xn, H[:, c, :], self.RSTD[:, c:c + 1], self.GBC, ALU.mult, ALU.mult)
            bank = self.banks[6 + (c % 2)][:].bitcast(BF16)
            for k in range(8):
                self.tr(bank[:, k * 128:(k + 1) * 128], xn[:, k * 128:(k + 1) * 128], self.IDENT[:])
            self.cp(self.XNT[:, :, c * 128:(c + 1) * 128], bank.rearrange("p (k t) -> p k t", k=8), eng="act")
            yield

    def final_norm(self, s_, raw=False):
        H = self.H
        yv = self.y[s_].rearrange("(c p) d -> p c d", p=128)
        outs = []
        if raw:
            for c in range(16):
                outs.append(self.dma(yv[:, c, :], H[:, c, :], q="sp", sb_out=False, sb_in=True))
            return outs
        self.dma(self.GBC, self.final_g.partition_broadcast(128), q="sp")
        junk = self.xn_tmp[:, 0:1024]
        for c in range(16):
            self.act(junk, H[:, c, :], AF.Square, accum_out=self.SSQ[:, c:c + 1])
        self.ts(self.RSTD[:], self.SSQ[:], 1.0 / D, EPS, ALU.mult, ALU.add)
        self.act(self.RSTD[:], self.RSTD[:], AF.Sqrt)
        self.recip(self.RSTD[:], self.RSTD[:])
        for c in range(16):
            self.stt(self.OUTT, H[:, c, :], self.RSTD[:, c:c + 1], self.GBC, ALU.mult, ALU.mult)
            outs.append(self.dma(yv[:, c, :], self.OUTT, q="sp", sb_out=False, sb_in=True))
        return outs

    def qkv_views(self, b, layer):
        base = 16384 + b * 8192
        QT = self.AR[:, base:base + 2048]
        KT = self.AR[:, base + 2048:base + 4096]
        vw = 192 if layer == 0 else 128
        V = self.AR[:, base + 4096:base + 4096 + 16 * vw].rearrange("p (c d) -> p c d", c=16)
        return QT, KT, V

    def load_inproj_w(self, w_in, cq, ck, cv, b):
        if "w" in os.environ.get("KSKIP", ""):
            return
        wv = w_in[0]
        for i, c0 in enumerate((cq, ck, cv)):
            self.dma(self.wslot(b * 3 + i), wv[:, c0:c0 + 128].rearrange("(k p) n -> p k n", p=128))

    def inproj(self, b, layer):
        QT, KT, V = self.qkv_views(b, layer)
        WQ, WK, WV = (self.wslot(b * 3 + i) for i in range(3))
        if layer == 0:
            self.memset(V[:, :, 64:128], 1.0)
        for g in range(4 if "v" not in os.environ.get("KSKIP", "") else 0):
            bank = self.banks[7]
            for cc in range(4):
                c = g * 4 + cc
                for k in range(8):
                    self.mm(bank[:, cc * 128:(cc + 1) * 128], self.XNT[:, k, c * 128:(c + 1) * 128], WV[:, k, :],
                            start=(k == 0), stop=(k == 7))
            bv = bank[:].rearrange("p (c d) -> p c d", c=4)
            if layer == 0:
                self.cp(V[:, g * 4:(g + 1) * 4, 0:64], bv[:, :, 0:64], eng="act")
                self.cp(V[:, g * 4:(g + 1) * 4, 128:192], bv[:, :, 64:128], eng="act")
            else:
                self.cp(V[:, g * 4:(g + 1) * 4, :], bv, eng="act")
            yield
        n = 0
        for W, DST in ((WK, KT), (WQ, QT)):
            for tt_ in range(4 if "q" not in os.environ.get("KSKIP", "") else 0):
                cols = slice(tt_ * 512, (tt_ + 1) * 512)
                acc = self.banks[4 + (n % 2)] if layer == 0 else self.banks[6]
                prm = self.banks[6] if layer == 0 else self.banks[7]
                raw = self.RAW[:, n % 2, :]
                n += 1
                for k in range(8):
                    self.mm(acc[:], W[:, k, :], self.XNT[:, k, cols], start=(k == 0), stop=(k == 7))
                self.cp(raw, acc[:], eng="act")
                self.mm(prm[:], self.PERM[:], raw)
                raw2 = self.next_pt()
                self.cp(raw2, prm[:], eng="act")
                t1 = self.TMP[:, 0, :]
                t2 = self.TMP[:, 1, :]
                self.tt(t1, raw, self.CT[:, cols], ALU.mult)
                self.tt(t2, raw2, self.ST[:, cols], ALU.mult)
                self.tt(DST[:, cols], t1, t2, ALU.add)
                yield

    def moba_gate(self, b):
        QT, KT, V = self.qkv_views(b, 0)
        self.red(self.KM[:], KT.rearrange("p (n t) -> p n t", n=8), ALU.add)
        self.ts(self.KMH[:], self.KM[:], 1.0 / 256, None, ALU.mult)
        self.cp(self.T8[:, 24:32], self.KMH[:])
        self.stt(self.KML[:], self.KM[:], 1.0 / 256, self.T8[:, 24:32], ALU.mult, ALU.subtract)
        self.memset(self.GM[:], -1e30)
        self.memset(self.SELB[:], 0.0)
        G = self.banks[7]
        TB = self.banks[6][:].bitcast(BF16)
        for c in range(8, 16):
            qb = c // 2
            for hh in range(2):
                ps = slice(hh * 64, (hh + 1) * 64)
                Gh = (G, self.banks[5])[hh]
                self.mm(Gh[:, 0:8], QT[ps, c * 128:(c + 1) * 128], self.KMH[ps, :], start=True, stop=False)
                self.mm(Gh[:, 0:8], QT[ps, c * 128:(c + 1) * 128], self.KML[ps, :], start=False, stop=True)
            for hh in range(2):
                Gh = (G, self.banks[5])[hh]
                self.cp(self.GM[:, hh * 8:hh * 8 + qb], Gh[:, 0:qb])
            for hh in range(2):
                self.max8(self.T8[:, hh * 8:(hh + 1) * 8], self.GM[:, hh * 8:(hh + 1) * 8])
            for hh in range(2):
                self.tt(self.T8[:, hh * 8 + 7:hh * 8 + 8], self.T8[:, hh * 8 + 2:hh * 8 + 3], self.T8[:, hh * 8 + 3:hh * 8 + 4], ALU.add)
                self.stt(self.T8[:, 16 + hh * 8:16 + hh * 8 + qb], self.GM[:, hh * 8:hh * 8 + qb], 2.0,
                         self.T8[:, hh * 8 + 7:hh * 8 + 8].to_broadcast([128, qb]), ALU.mult, ALU.is_lt)
                self.ts(self.SELB[:, hh * 64:hh * 64 + qb], self.T8[:, 16 + hh * 8:16 + hh * 8 + qb], -BIG, None, ALU.mult)
            self.tr(TB[:, 0:128], self.SELB[:], self.IDENT[:])
            self.cp(self.SELBT[:, c * 128:(c + 1) * 128], TB[:, 0:128], eng="act")
            yield

    def attn0(self, b, mixer, ot_chunk):
        QT, KT, V = self.qkv_views(b, 0)
        it = 0
        for hh in range(2):
            ps = slice(hh * 64, (hh + 1) * 64)
            for qt in range(4):
                O = self.banks[2 + (it % 2)]
                nk = 4 * qt + 4
                for kc in range(nk):
                    Sb = self.banks[it % 2 if False else (kc % 2)]
                    col0 = max(0, kc * 128 - qt * 512)
                    qs = slice(qt * 512 + col0, (qt + 1) * 512)
                    cs = slice(col0, 512)
                    need_bias = (mixer == "B" and qt >= 2 and "b" not in os.environ.get("KSKIP", ""))
                    self.mm(Sb[:, cs], KT[ps, kc * 128:(kc + 1) * 128], QT[ps, qs], start=True, stop=not need_bias)
                    if need_bias:
                        r = hh * 64 + kc // 2
                        oh = self.IDENT[ps, r:r + 1].to_broadcast([64, 128])
                        self.mm(Sb[:, cs], oh, self.SELBT[ps, qs], start=False, stop=True)
                    pt = self.next_pt()
                    self.act(pt[:, cs], Sb[:, cs], AF.Exp, scale=0.125)
                    if mixer == "A":
                        off = 128 * (4 * qt - kc) + 384
                        self.tt(pt[:, cs], pt[:, cs], self.TM[:, off + col0:off + 512], ALU.mult)
                    elif kc >= 4 * qt:
                        self.tt(pt[:, col0:col0 + 128], pt[:, col0:col0 + 128], self.TRI[:], ALU.mult)
                    lhs = V[:, kc, 0:128] if hh == 0 else V[:, kc, 64:192]
                    self.mm(O[:, cs], lhs, pt[:, cs], start=(kc == 0), stop=(kc == nk - 1))
                qcols = slice(qt * 512, (qt + 1) * 512)
                rc = self.TMP[:, 0, :]
                if hh == 0:
                    self.recip(rc[0:64, :], O[64:128, :], fast=True)
                    self.tt(self.OT[0:64, ot_chunk, qcols], rc[0:64, :], O[0:64, :], ALU.mult)
                else:
                    self.recip(rc[64:128, :], O[0:64, :], fast=True)
                    self.tt(self.OT[64:128, ot_chunk, qcols], rc[64:128, :], O[64:128, :], ALU.mult)
                it += 1
                yield

    def attn1(self, b, h):
        QT, KT, V = self.qkv_views(b, 1)
        for qt in range(4):
            nk = 4 * qt + 4
            qcols = slice(qt * 512, (qt + 1) * 512)
            OB = (self.banks[2], self.banks[4])
            DB = (self.banks[3], self.banks[5])
            for i in range(2):
                ps = slice(i * 64, (i + 1) * 64)
                for kc in range(nk):
                    Sb = self.banks[kc % 2]
                    col0 = max(0, kc * 128 - qt * 512)
                    qs = slice(qt * 512 + col0, (qt + 1) * 512)
                    cs = slice(col0, 512)
                    self.mm(Sb[:, cs], KT[ps, kc * 128:(kc + 1) * 128], QT[ps, qs])
                    pt = self.next_pt()
                    self.act(pt[:, cs], Sb[:, cs], AF.Exp, scale=0.125)
                    if kc >= 4 * qt:
                        self.tt(pt[:, col0:col0 + 128], pt[:, col0:col0 + 128], self.TRI[:], ALU.mult)
                    self.mm(OB[i][:, cs], V[:, kc, :], pt[:, cs], start=(kc == 0), stop=(kc == nk - 1))
                    self.mm(DB[i][:, cs], self.ONES[:], pt[:, cs], start=(kc == 0), stop=(kc == nk - 1))
                yield
            t1 = self.TMP[:, 0, :]
            t2 = self.TMP[:, 1, :]
            self.recip(t1, DB[0][:], fast=True)
            self.tt(t1, t1, OB[0][:], ALU.mult)
            self.recip(t2, DB[1][:], fast=True)
            self.tt(t2, t2, OB[1][:], ALU.mult)
            self.stt(t1, t2, self.NEGLAM, t1, ALU.mult, ALU.add)
            osq = self.next_pt()
            self.act(osq, t1, AF.Square)
            SSb = self.banks[7]
            self.mm(SSb[:], self.ONES[:], osq)
            self.ts(t2, SSb[:], 1.0 / 128, EPS, ALU.mult, ALU.add)
            self.act(t2, t2, AF.Ln)
            self.act(t2, t2, AF.Exp, scale=-0.5)
            self.stt(self.OT[:, h, qcols], t1, self.GS[:, 0:1], t2, ALU.mult, ALU.mult)
            yield

    def outproj(self, w_out):
        wv = w_out[0]
        slots = (6, 0)
        for nh in range(2):
            s0 = slots[nh]
            WO = self.WB[:, s0:s0 + 4, :].rearrange("p s n -> p (s n)").rearrange("p (k n) -> p k n", k=8)
            self.dma(WO, wv[:, nh * 512:(nh + 1) * 512].rearrange("(k p) n -> p k n", p=128))
            for c in range(16):
                bank = self.banks[4 + (c % 4)]
                for k in range(8):
                    self.mm(bank[:], self.OT[:, k, c * 128:(c + 1) * 128], WO[:, k, :], start=(k == 0), stop=(k == 7))
                hs = self.H[:, c, nh * 512:(nh + 1) * 512]
                self.tt(hs, hs, bank[:], ALU.add)
                yield

    def ffn(self, l):
        wg = self.w_gate[l]
        wu = self.w_up[l]
        wd = self.w_down[l]
        HID = self.HID
        for th in range(2):
            t0 = th * 1024
            pend = []

            def load_gu(j):
                s0 = (2 * j) % NSLOT
                self.dma(self.wslot(s0), wg[:, j * 128:(j + 1) * 128].rearrange("(k p) n -> p k n", p=128))
                self.dma(self.wslot(s0 + 1), wu[:, j * 128:(j + 1) * 128].rearrange("(k p) n -> p k n", p=128))
            PRE = 4
            for j in range(min(PRE, NJ)):
                load_gu(j)
            n = 0
            for j in range(NJ):
                s0 = (2 * j) % NSLOT
                WG = self.wslot(s0)
                WU = self.wslot(s0 + 1)
                for t2 in range(2):
                    cols = slice(t0 + t2 * 512, t0 + (t2 + 1) * 512)
                    Bg = self.banks[n % 2]
                    Bu = self.banks[2 + (n % 2)]
                    n += 1
                    for k in range(8):
                        self.mm(Bg[:], WG[:, k, :], self.XNT[:, k, cols], start=(k == 0), stop=(k == 7))
                    for k in range(8):
                        self.mm(Bu[:], WU[:, k, :], self.XNT[:, k, cols], start=(k == 0), stop=(k == 7))
                    sg = self.TMP[:, n % 2, :]
                    self.act(sg, Bg[:], AF.Silu)
                    self.tt(HID[:, j, t2 * 512:(t2 + 1) * 512], sg, Bu[:], ALU.mult)
                if j + PRE < NJ:
                    load_gu(j + PRE)
                yield
            for nh in range(2):
                def load_d(j, nh=nh):
                    self.dma(self.WB[:, j % NSLOT, 0:512], wd[j * 128:(j + 1) * 128, nh * 512:(nh + 1) * 512])
                PD = NSLOT - 1
                for j in range(PD):
                    load_d(j)
                for j in range(NJ):
                    WD = self.WB[:, j % NSLOT, 0:512]
                    for cc in range(8):
                        self.mm(self.banks[cc][:], HID[:, j, cc * 128:(cc + 1) * 128], WD, start=(j == 0), stop=(j == NJ - 1))
                    if j + PD < NJ:
                        load_d(j + PD)
                    if j % 4 == 3:
                        yield
                for cc in range(8):
                    c = t0 // 128 + cc
                    hs = self.H[:, c, nh * 512:(nh + 1) * 512]
                    self.tt(hs, hs, self.banks[cc][:], ALU.add)
                yield

    def run(self, gen):
        for _ in gen:
            pass

    def run_interleaved(self, main, side, ratio=2):
        side_done = side is None
        for _ in main:
            if not side_done:
                for _r in range(ratio):
                    try:
                        next(side)
                    except StopIteration:
                        side_done = True
                        break
        if not side_done:
            for _ in side:
                pass

    def layer0_mixer_v2(self):
        self.run(self.norm_to_xnt(self.ab_norm_g[0:1, :]))
        chunks = []
        for j in range(4):
            chunks.append(("A", j * 128, 512 + j * 128, 1024 + j * 128, j))
        for j in range(4):
            chunks.append(("B", 1536 + j * 128, 2048 + j * 128, 2560 + j * 128, 4 + j))
        n = len(chunks)
        for i in range(2):
            self.load_inproj_w(self.ab_w_in, chunks[i][1], chunks[i][2], chunks[i][3], i)
        self.dma(self.TM, self.c_tm, q="sp")
        self.run(self.inproj(0, 0))
        for i, (mixer, cq, ck, cv, otc) in enumerate(chunks):
            b = i % 2
            if mixer == "B":
                self.run(self.moba_gate(b))
            side = self.inproj(1 - b, 0) if i + 1 < n else None
            self.run_interleaved(self.attn0(b, mixer, otc), side, ratio=2)
            if i + 2 < n:
                nx = chunks[i + 2]
                self.load_inproj_w(self.ab_w_in, nx[1], nx[2], nx[3], b)
        self.run(self.outproj(self.ab_w_out))

    def layer1_mixer_v2(self):
        self.run(self.norm_to_xnt(self.diff_norm_g[0:1, :]))
        for h in range(2):
            self.load_inproj_w(self.diff_w_in, h * 128, 1024 + h * 128, 2048 + h * 128, h)
        self.run(self.inproj(0, 1))
        for h in range(8):
            b = h % 2
            side = self.inproj(1 - b, 1) if h + 1 < 8 else None
            self.run_interleaved(self.attn1(b, h), side, ratio=1)
            if h + 2 < 8:
                self.load_inproj_w(self.diff_w_in, (h + 2) * 128, 1024 + (h + 2) * 128, 2048 + (h + 2) * 128, b)
        self.run(self.outproj(self.diff_w_out))

    def layer0_mixer(self):
        self.run(self.norm_to_xnt(self.ab_norm_g[0:1, :]))
        chunks = []
        for j in range(4):
            chunks.append(("A", j * 128, 512 + j * 128, 1024 + j * 128, j))
        for j in range(4):
            chunks.append(("B", 1536 + j * 128, 2048 + j * 128, 2560 + j * 128, 4 + j))
        self.load_inproj_w(self.ab_w_in, chunks[0][1], chunks[0][2], chunks[0][3], 0)
        cur_scr = None
        kdbg = int(os.environ.get("KDBG", "99"))
        for i, (mixer, cq, ck, cv, otc) in enumerate(chunks):
            b = i % 2
            if i > kdbg:
                break
            if i + 1 < len(chunks):
                nx = chunks[i + 1]
                self.load_inproj_w(self.ab_w_in, nx[1], nx[2], nx[3], 1 - b)
            if "i" not in os.environ.get("KSKIP", ""):
                self.run(self.inproj(b, 0))
            if i == kdbg:
                break
            if mixer == "A" and cur_scr != "TM":
                self.dma(self.TM, self.c_tm, q="sp")
                cur_scr = "TM"
            if mixer == "B":
                cur_scr = "SELB"
                self.run(self.moba_gate(b))
                if os.environ.get("KDUMP") and i == 4:
                    yv = self.y[0]
                    self.dbg_outs = [self.dma(yv[0:128, 0:16], self.GM[:], q="sp", sb_out=False, sb_in=True),
                                     self.dma(yv[0:128, 16:48], self.T8[:], q="sp", sb_out=False, sb_in=True),
                                     self.dma(yv[0:128, 48:56], self.KM[:], q="sp", sb_out=False, sb_in=True)]
                    return
            self.run(self.attn0(b, mixer, otc))
        if "o" not in os.environ.get("KSKIP", ""):
            self.run(self.outproj(self.ab_w_out))

    def layer1_mixer(self):
        self.run(self.norm_to_xnt(self.diff_norm_g[0:1, :]))
        self.load_inproj_w(self.diff_w_in, 0, 1024, 2048, 0)
        for h in range(8):
            b = h % 2
            if h + 1 < 8:
                self.load_inproj_w(self.diff_w_in, (h + 1) * 128, 1024 + (h + 1) * 128, 2048 + (h + 1) * 128, 1 - b)
            self.run(self.inproj(b, 1))
            self.run(self.attn1(b, h))
        self.run(self.outproj(self.diff_w_out))

    def build(self):
        self.load_consts()
        outs = []
        stages = ["attn0", "l0", "attn1", "l1", None]
        stop = self.stop_after
        for s_ in range(2):
            self.dma(self.H[:], self.x[s_].rearrange("(c p) d -> p c d", p=128), q="sp")
            if stop == "x":
                outs += self.final_norm(s_, raw=True)
                continue
            self.rope_tables(s_)
            if stop == "rope":
                outs += self.final_norm(s_, raw=True)
                continue
            if stop == "norm0":
                self.run(self.norm_to_xnt(self.ab_norm_g[0:1, :]))
                outs += self.final_norm(s_, raw=True)
                continue
            done = False
            for stage in stages:
                if stage == "attn0":
                    if os.environ.get("KOLD"):
                        self.layer0_mixer()
                    else:
                        self.layer0_mixer_v2()
                    if os.environ.get("KDUMP"):
                        self.R.add("sp", None, extra_deps=self.dbg_outs)
                        return
                elif stage == "l0":
                    self.run(self.norm_to_xnt(self.ffn_norm_g[0:1, :]))
                    self.run(self.ffn(0))
                elif stage == "attn1":
                    if os.environ.get("KOLD"):
                        self.layer1_mixer()
                    else:
                        self.layer1_mixer_v2()
                elif stage == "l1":
                    self.run(self.norm_to_xnt(self.ffn_norm_g[1:2, :]))
                    self.run(self.ffn(1))
                if stage == stop:
                    outs += self.final_norm(s_, raw=(stage is not None))
                    done = True
                    break
            assert done
        self.R.add("sp", None, extra_deps=outs)


def build_nc(stop_after=None):
    _reg_cache.clear()
    nc = bass.Bass("TRN2", target_bir_lowering=False)
    with ExitStack() as stack:
        P = Prog(nc, stack, stop_after=stop_after)
        P.build()
        P.R.finalize(nc, stack)
        block = stack.enter_context(nc.Block())

        @block.tensor
        def _(e):
            P.R.emit("pe", e)

        @block.scalar
        def _(e):
            P.R.emit("act", e)

        @block.vector
        def _(e):
            P.R.emit("dve", e)

        @block.gpsimd
        def _(e):
            P.R.emit("pool", e)

        @block.sync
        def _(e):
            P.R.emit("sp", e)
    return nc


def make_consts():
    bf = ml_dtypes.bfloat16
    ident = np.eye(128, dtype=np.float32).astype(bf)
    perm = np.zeros((128, 128), dtype=np.float32)
    invf = np.zeros((128, 1), dtype=np.float32)
    inv_freq = (np.float32(500000.0) ** (-np.arange(0, 16, 2, dtype=np.float32) / np.float32(16))).astype(np.float32)
    for d_ in range(128):
        dd = d_ % 64
        if dd < 8:
            perm[d_ + 8, d_] = 1.0
            invf[d_, 0] = -inv_freq[dd]
        elif dd < 16:
            perm[d_ - 8, d_] = 1.0
            invf[d_, 0] = inv_freq[dd - 8]
    kk = np.arange(128)[:, None]
    qq = np.arange(128)[None, :]
    tri = (qq >= kk).astype(np.float32).astype(bf)
    jj = np.arange(TMW)[None, :]
    dist = jj - kk - 384
    cnt = ((dist >= 0) & (dist <= 128)).astype(np.float32)
    cnt += ((dist >= 0) & (dist % 4 == 0) & (dist <= 512)).astype(np.float32)
    cnt += ((dist >= 0) & (dist % 16 == 0) & (dist <= 2048)).astype(np.float32)
    return {"c_ident": ident, "c_perm": perm.astype(bf), "c_tri": tri, "c_tm": cnt.astype(bf), "c_invf": invf}


_NC_CACHE = {}


def kernel(stop_after=None, **inputs):
    key = stop_after
    if key not in _NC_CACHE:
        _NC_CACHE[key] = build_nc(stop_after)
    nc = _NC_CACHE[key]
    consts = make_consts()
    x = np.ascontiguousarray(inputs["x"], dtype=np.float32)
    pos = np.ascontiguousarray(inputs["positions"], dtype=np.int32)
    shared = {k: np.ascontiguousarray(v) for k, v in inputs.items() if k not in ("x", "positions")}
    shared["final_norm_g"] = shared["final_norm_g"].reshape(1, D)
    shared.update(consts)
    in_maps = []
    ncores = int(os.environ.get("KCORES", "8"))
    for c in range(ncores):
        m = dict(shared)
        m["x"] = x[2 * c:2 * c + 2]
        m["positions"] = pos[2 * c:2 * c + 2]
        in_maps.append(m)
    res = run_bass_kernel_spmd(nc, in_maps, core_ids=list(range(ncores)))
    return np.concatenate([np.asarray(r["y"]) for r in res.results], axis=0).astype(np.float32)
```

```python
import math
import os
import itertools
from contextlib import ExitStack

import numpy as np
import ml_dtypes

import concourse.bass as bass
import concourse.mybir as mybir
from concourse.bass_utils import run_bass_kernel_spmd

F32 = mybir.dt.float32
BF16 = mybir.dt.bfloat16
I32 = mybir.dt.int32
ALU = mybir.AluOpType
AF = mybir.ActivationFunctionType
AX = mybir.AxisListType

S = 2048
D = 1024
DFF = 2816
NJ = DFF // 128
EPS = 1e-5
BIG = 30000.0
LAMBDA_INIT = 0.8 - 0.6 * math.exp(-0.3 * 1)
TWO_PI = 2.0 * math.pi
CW1 = 6.28125
CW2 = TWO_PI - CW1
MAGIC = 12582912.0
PI_SAFE = 3.1415925
TMW = 2432
NSLOT = 10
BLK = 256


_reg_cache = {}


def _region(ap):
    key = (ap.tensor.name, ap.offset, tuple(ap.ap), str(ap.dtype))
    r = _reg_cache.get(key)
    if r is not None:
        return r
    es = mybir.dt.size(ap.dtype)
    T = 1
    for s_ in ap.tensor.shape[1:]:
        T *= s_
    col = ap.offset % T
    free = sorted([(st, n) for st, n in list(ap.ap)[1:] if n > 1 and st != 0])
    run = 1
    if free and free[0][0] == 1:
        run = free[0][1]
        free = free[1:]
    while free and free[0][0] == run:
        run *= free[0][1]
        free = free[1:]
    cnt = 1
    for _, n in free:
        cnt *= n
    ivs = []
    if cnt <= 64:
        for combo in itertools.product(*[range(n) for _, n in free]):
            st = col + sum(c * f[0] for c, f in zip(combo, free))
            ivs.append((st, st + run))
    else:
        hi = col + sum((n - 1) * st for st, n in free) + run
        ivs.append((col, hi))
    blocks = set()
    name = ap.tensor.name
    for lo, hi in ivs:
        for b in range((lo * es) // BLK, (hi * es - 1) // BLK + 1):
            blocks.add((name, b))
    r = tuple(blocks)
    _reg_cache[key] = r
    return r


class Op:
    __slots__ = ("eng", "fn", "rd", "wr", "dma", "deps", "sig", "sigidx", "waits", "dsem", "dval", "idx", "has_dep")


class Rec:
    ENGS = ("pe", "act", "dve", "pool", "sp")

    def __init__(self):
        self.ops = []
        self.last_writer = {}
        self.readers = {}

    def add(self, eng, fn, reads=(), writes=(), dma=False, extra_deps=()):
        op = Op()
        op.eng = eng
        op.fn = fn
        op.dma = dma
        op.idx = len(self.ops)
        op.has_dep = False
        deps = set(extra_deps)
        rd = set()
        wr = set()
        for ap in reads:
            if ap is not None:
                rd.update(_region(ap))
        for ap in writes:
            if ap is not None:
                wr.update(_region(ap))
        lw = self.last_writer
        rdrs = self.readers
        for b in rd:
            w = lw.get(b)
            if w is not None:
                deps.add(w)
        for b in wr:
            w = lw.get(b)
            if w is not None:
                deps.add(w)
            rr = rdrs.get(b)
            if rr:
                deps.update(rr.values())
        for b in wr:
            lw[b] = op.idx
            rdrs[b] = {}
        rkey = eng if not dma else ("dma", op.idx)
        for b in rd:
            if b in wr:
                continue
            d_ = rdrs.get(b)
            if d_ is None:
                d_ = rdrs[b] = {}
            d_[rkey] = op.idx
        deps.discard(op.idx)
        op.deps = deps
        self.ops.append(op)
        return op.idx

    def finalize(self, nc, stack):
        ops = self.ops
        for op in ops:
            keep = set()
            for d_ in op.deps:
                p = ops[d_]
                if p.eng == "pe" and op.eng == "pe" and not p.dma and not op.dma:
                    continue
                keep.add(d_)
                p.has_dep = True
            op.deps = keep
        engsem = {e: stack.enter_context(nc.semaphore("sem_" + e)) for e in self.ENGS}
        KD = 8
        dmasem = {e: [stack.enter_context(nc.semaphore("dsem_%s%d" % (e, i))) for i in range(KD)] for e in ("pool", "sp")}
        cnt = {e: 0 for e in self.ENGS}
        dcnt = {"pool": 0, "sp": 0}
        for op in ops:
            if op.dma:
                j = dcnt[op.eng]
                dcnt[op.eng] += 1
                op.dsem = dmasem[op.eng][j % KD]
                op.dval = 16 * (j // KD + 1)
                op.sig = False
            else:
                op.sig = op.has_dep and op.fn is not None
                if op.sig:
                    cnt[op.eng] += 1
                    op.sigidx = cnt[op.eng]
        maxw = {e: {} for e in self.ENGS}
        for op in ops:
            need = {}
            if op.dma and op.dval > 16:
                need[op.dsem] = op.dval - 16
            for d_ in op.deps:
                p = ops[d_]
                if p.dma:
                    sem, val = p.dsem, p.dval
                else:
                    sem, val = engsem[p.eng], p.sigidx
                if need.get(sem, 0) < val:
                    need[sem] = val
            mw = maxw[op.eng]
            waits = []
            for sem, val in need.items():
                if mw.get(sem, 0) < val:
                    mw[sem] = val
                    waits.append((sem, val))
            op.waits = waits
        self.engsem = engsem

    def emit(self, eng, e):
        sem_e = self.engsem[eng]
        for op in self.ops:
            if op.eng != eng:
                continue
            for sem, val in op.waits:
                e.wait_ge(sem, val)
            if op.fn is None:
                continue
            ins = op.fn(e)
            if op.dma:
                ins.then_inc(op.dsem, 16)
            elif op.sig:
                ins.then_inc(sem_e, 1)


class Prog:
    def __init__(self, nc, stack, stop_after=None):
        self.nc = nc
        self.R = Rec()
        self.stop_after = stop_after
        st = stack
        d = lambda name, shape, dt, kind="ExternalInput": nc.dram_tensor(name, shape, dt, kind=kind).ap()
        self.x = d("x", [2, S, D], F32)
        self.pos = d("positions", [2, S], I32)
        self.ab_norm_g = d("ab_norm_g", [1, D], F32)
        self.ab_w_in = d("ab_w_in", [1, D, 3072], F32)
        self.ab_w_out = d("ab_w_out", [1, D, D], F32)
        self.diff_norm_g = d("diff_norm_g", [1, D], F32)
        self.diff_w_in = d("diff_w_in", [1, D, 3072], F32)
        self.diff_w_out = d("diff_w_out", [1, D, D], F32)
        self.lq1 = d("diff_lambda_q1", [1, 64], F32)
        self.lk1 = d("diff_lambda_k1", [1, 64], F32)
        self.lq2 = d("diff_lambda_q2", [1, 64], F32)
        self.lk2 = d("diff_lambda_k2", [1, 64], F32)
        self.subln_g = d("diff_subln_g", [1, 128], F32)
        self.ffn_norm_g = d("ffn_norm_g", [2, D], F32)
        self.w_gate = d("ffn_w_gate", [2, D, DFF], F32)
        self.w_up = d("ffn_w_up", [2, D, DFF], F32)
        self.w_down = d("ffn_w_down", [2, DFF, D], F32)
        self.final_g = d("final_norm_g", [1, D], F32)
        self.c_ident = d("c_ident", [128, 128], BF16)
        self.c_perm = d("c_perm", [128, 128], BF16)
        self.c_tri = d("c_tri", [128, 128], BF16)
        self.c_tm = d("c_tm", [128, TMW], BF16)
        self.c_invf = d("c_invf", [128, 1], F32)
        self.y = d("y", [2, S, D], F32, kind="ExternalOutput")

        sb = lambda name, shape, dt: st.enter_context(nc.sbuf_tensor(name, shape, dt))
        self.H = sb("H", [128, 16, D], F32)
        self.XNT = sb("XNT", [128, 8, S], BF16)
        self.AR = sb("AR", [128, 32768], BF16)
        self.WB = sb("WB", [128, NSLOT, 1024], BF16)
        self.CT = sb("CT", [128, S], BF16)
        self.ST = sb("ST", [128, S], BF16)
        self.PT = sb("PT", [128, 3, 512], BF16)
        self.RAW = sb("RAW", [128, 2, 512], BF16)
        self.TMP = sb("TMP", [128, 2, 512], F32)
        self.SCR = sb("SCR", [128, 2048], F32)
        self.IDENT = sb("IDENT", [128, 128], BF16)
        self.PERM = sb("PERM", [128, 128], BF16)
        self.TRI = sb("TRI", [128, 128], BF16)
        self.ONES = sb("ONES", [128, 128], BF16)
        self.INVF = sb("INVF", [128, 1], F32)
        self.SSQ = sb("SSQ", [128, 16], F32)
        self.RSTD = sb("RSTD", [128, 16], F32)
        self.KM = sb("KM", [128, 8], F32)
        self.KMH = sb("KMH", [128, 8], BF16)
        self.KML = sb("KML", [128, 8], BF16)
        self.GM = sb("GM", [128, 16], F32)
        self.T8 = sb("T8", [128, 32], F32)
        self.SELB = sb("SELB", [128, 128], BF16)
        self.LAM = sb("LAM", [128, 4, 64], F32)
        self.LSC = sb("LSC", [128, 8], F32)
        self.GS = sb("GS", [128, 1], F32)
        self.banks = [st.enter_context(nc.psum_tensor("PS%d" % i, [128, 512], F32)) for i in range(8)]

        self.OT = self.AR[:, 0:16384].rearrange("p (k t) -> p k t", k=8)
        self.HID = self.AR[:, 0:NJ * 1024].rearrange("p (j t) -> p j t", j=NJ)
        self.scr_bf = self.SCR[:].bitcast(BF16)
        self.GBC = self.SCR[:, 0:1024]
        self.OUTT = self.SCR[:, 1024:2048]
        self.JUNK = self.scr_bf[:, 2048:3072]
        self.TM = self.scr_bf[:, 0:TMW]
        self.SELBT = self.scr_bf[:, 0:2048]
        self.xn_tmp = self.TMP[:].rearrange("p a b -> p (a b)").bitcast(BF16)
        self.pt_i = 0
        self.slot_i = 0

    def mm(self, out, lhsT, rhs, start=True, stop=True):
        self.R.add("pe", lambda e: e.matmul(out, lhsT, rhs, start=start, stop=stop), reads=(lhsT, rhs), writes=(out,))

    def tr(self, out, in_, ident):
        self.R.add("pe", lambda e: e.transpose(out, in_, ident), reads=(in_, ident), writes=(out,))

    def act(self, out, in_, func, bias=None, scale=None, accum_out=None):
        kw = {}
        rd = [in_]
        if bias is not None:
            kw["bias"] = bias
            if not isinstance(bias, (int, float)):
                rd.append(bias)
        if scale is not None:
            kw["scale"] = scale
            if not isinstance(scale, (int, float)):
                rd.append(scale)
        wr = [out]
        if accum_out is not None:
            kw["accum_out"] = accum_out
            wr.append(accum_out)
        self.R.add("act", lambda e: e.activation(out, in_, func, **kw), reads=rd, writes=wr)

    def tt(self, out, in0, in1, op, eng="dve"):
        self.R.add(eng, lambda e: e.tensor_tensor(out, in0, in1, op), reads=(in0, in1), writes=(out,))

    def ts(self, out, in0, s1, s2, op0, op1=None, eng="dve"):
        rd = [in0]
        for s_ in (s1, s2):
            if s_ is not None and not isinstance(s_, (int, float)):
                rd.append(s_)
        if op1 is None:
            self.R.add(eng, lambda e: e.tensor_scalar(out, in0, s1, None, op0), reads=rd, writes=(out,))
        else:
            self.R.add(eng, lambda e: e.tensor_scalar(out, in0, s1, s2, op0, op1), reads=rd, writes=(out,))

    def stt(self, out, in0, scalar, in1, op0, op1, eng="dve"):
        rd = [in0, in1]
        if not isinstance(scalar, (int, float)):
            rd.append(scalar)
        self.R.add(eng, lambda e: e.scalar_tensor_tensor(out, in0, scalar, in1, op0, op1), reads=rd, writes=(out,))

    def cp(self, out, in_, eng="dve"):
        if eng == "act":
            self.R.add("act", lambda e: e.copy(out, in_), reads=(in_,), writes=(out,))
        else:
            self.R.add(eng, lambda e: e.tensor_copy(out, in_), reads=(in_,), writes=(out,))

    def memset(self, out, val, eng="dve"):
        self.R.add(eng, lambda e: e.memset(out, val), writes=(out,))

    def recip(self, out, in_, fast=False):
        if fast:
            self.act(out, in_, AF.Ln)
            self.act(out, out, AF.Exp, scale=-1.0)
        else:
            self.R.add("dve", lambda e: e.reciprocal(out, in_), reads=(in_,), writes=(out,))

    def red(self, out, in_, op):
        self.R.add("dve", lambda e: e.tensor_reduce(out, in_, AX.X, op), reads=(in_,), writes=(out,))

    def max8(self, out, in_):
        self.R.add("dve", lambda e: e.max(out, in_), reads=(in_,), writes=(out,))

    def dma(self, out, in_, q="pool", sb_out=True, sb_in=False):
        return self.R.add(q, lambda e: e.dma_start(out=out, in_=in_), reads=((in_,) if sb_in else ()),
                          writes=((out,) if sb_out else ()), dma=True)

    def next_pt(self):
        i = self.pt_i
        self.pt_i = (i + 1) % 3
        return self.PT[:, i, :]

    def wslot(self, s_):
        return self.WB[:, s_, :].rearrange("p (k n) -> p k n", k=8)

    def load_consts(self):
        self.dma(self.IDENT[:], self.c_ident, q="sp")
        self.dma(self.PERM[:], self.c_perm, q="sp")
        self.dma(self.TRI[:], self.c_tri, q="sp")
        self.dma(self.INVF[:], self.c_invf, q="sp")
        self.memset(self.ONES[:], 1.0)
        for i, v in enumerate((self.lq1, self.lk1, self.lq2, self.lk2)):
            self.dma(self.LAM[:, i, :], v.partition_broadcast(128), q="sp")
        self.dma(self.GS[:], self.subln_g.rearrange("o d -> d o"), q="sp")
        L = self.LSC
        self.tt(self.LAM[:, 0, :], self.LAM[:, 0, :], self.LAM[:, 1, :], ALU.mult)
        self.tt(self.LAM[:, 2, :], self.LAM[:, 2, :], self.LAM[:, 3, :], ALU.mult)
        self.red(L[:, 0:1], self.LAM[:, 0, :], ALU.add)
        self.red(L[:, 1:2], self.LAM[:, 2, :], ALU.add)
        self.act(L[:, 2:4], L[:, 0:2], AF.Exp)
        self.tt(L[:, 4:5], L[:, 3:4], L[:, 2:3], ALU.subtract)
        self.ts(L[:, 5:6], L[:, 4:5], -LAMBDA_INIT, None, ALU.add)
        self.ts(self.GS[:], self.GS[:], 1.0 - LAMBDA_INIT, None, ALU.mult)
        self.NEGLAM = L[:, 5:6]

    def rope_tables(self, s_):
        arf = self.AR[:].bitcast(F32)
        posi = self.AR[:].bitcast(I32)[:, 0:2048]
        ang = arf[:, 2048:4096]
        t_ = arf[:, 4096:6144]
        r_ = arf[:, 6144:8192]
        self.dma(posi, self.pos[s_:s_ + 1, :].partition_broadcast(128), q="sp")
        posf = arf[:, 8192:10240]
        self.cp(posf, posi)
        self.ts(ang, posf, self.INVF[:, 0:1], None, ALU.mult)
        self.ts(t_, ang, 1.0 / TWO_PI, MAGIC, ALU.mult, ALU.add)
        self.ts(t_, t_, -MAGIC, None, ALU.add)
        self.stt(r_, t_, -CW1, ang, ALU.mult, ALU.add)
        self.stt(r_, t_, -CW2, r_, ALU.mult, ALU.add)
        self.ts(r_, r_, -PI_SAFE, PI_SAFE, ALU.max, ALU.min)
        self.act(self.ST[:], r_, AF.Sin)
        self.stt(t_, r_, -1.0, r_, ALU.mult, ALU.max)
        self.ts(t_, t_, -1.0, math.pi / 2, ALU.mult, ALU.add)
        self.act(self.CT[:], t_, AF.Sin)

    def norm_to_xnt(self, gain_ap):
        H = self.H
        self.dma(self.GBC, gain_ap.partition_broadcast(128), q="sp")
        for c in range(16):
            self.act(self.JUNK, H[:, c, :], AF.Square, accum_out=self.SSQ[:, c:c + 1])
        self.ts(self.RSTD[:], self.SSQ[:], 1.0 / D, EPS, ALU.mult, ALU.add)
        self.act(self.RSTD[:], self.RSTD[:], AF.Sqrt)
        self.recip(self.RSTD[:], self.RSTD[:])
        for c in range(16):
            xn = self.xn_tmp[:, (c % 2) * 1024:(c % 2 + 1) * 1024]
            self.stt(xn, H[:, c, :], self.RSTD[:, c:c + 1], self.GBC, ALU.mult, ALU.mult)
            bank = self.banks[6 + (c % 2)][:].bitcast(BF16)
            for k in range(8):
                self.tr(bank[:, k * 128:(k + 1) * 128], xn[:, k * 128:(k + 1) * 128], self.IDENT[:])
            self.cp(self.XNT[:, :, c * 128:(c + 1) * 128], bank.rearrange("p (k t) -> p k t", k=8), eng="act")
            yield

    def final_norm(self, s_, raw=False):
        H = self.H
        yv = self.y[s_].rearrange("(c p) d -> p c d", p=128)
        outs = []
        if raw:
            for c in range(16):
                outs.append(self.dma(yv[:, c, :], H[:, c, :], q="sp", sb_out=False, sb_in=True))
            return outs
        self.dma(self.GBC, self.final_g.partition_broadcast(128), q="sp")
        junk = self.xn_tmp[:, 0:1024]
        for c in range(16):
            self.act(junk, H[:, c, :], AF.Square, accum_out=self.SSQ[:, c:c + 1])
        self.ts(self.RSTD[:], self.SSQ[:], 1.0 / D, EPS, ALU.mult, ALU.add)
        self.act(self.RSTD[:], self.RSTD[:], AF.Sqrt)
        self.recip(self.RSTD[:], self.RSTD[:])
        for c in range(16):
            self.stt(self.OUTT, H[:, c, :], self.RSTD[:, c:c + 1], self.GBC, ALU.mult, ALU.mult)
            outs.append(self.dma(yv[:, c, :], self.OUTT, q="sp", sb_out=False, sb_in=True))
        return outs

    def qkv_views(self, b, layer):
        base = 16384 + b * 8192
        QT = self.AR[:, base:base + 2048]
        KT = self.AR[:, base + 2048:base + 4096]
        vw = 192 if layer == 0 else 128
        V = self.AR[:, base + 4096:base + 4096 + 16 * vw].rearrange("p (c d) -> p c d", c=16)
        return QT, KT, V

    def load_inproj_w(self, w_in, cq, ck, cv, b):
        if "w" in os.environ.get("KSKIP", ""):
            return
        wv = w_in[0]
        for i, c0 in enumerate((cq, ck, cv)):
            self.dma(self.wslot(b * 3 + i), wv[:, c0:c0 + 128].rearrange("(k p) n -> p k n", p=128))

    def inproj(self, b, layer):
        QT, KT, V = self.qkv_views(b, layer)
        WQ, WK, WV = (self.wslot(b * 3 + i) for i in range(3))
        if layer == 0:
            self.memset(V[:, :, 64:128], 1.0)
        for g in range(4 if "v" not in os.environ.get("KSKIP", "") else 0):
            bank = self.banks[5] if layer == 0 else self.banks[7]
            for cc in range(4):
                c = g * 4 + cc
                for k in range(8):
                    self.mm(bank[:, cc * 128:(cc + 1) * 128], self.XNT[:, k, c * 128:(c + 1) * 128], WV[:, k, :],
                            start=(k == 0), stop=(k == 7))
            bv = bank[:].rearrange("p (c d) -> p c d", c=4)
            if layer == 0:
                self.cp(V[:, g * 4:(g + 1) * 4, 0:64], bv[:, :, 0:64], eng="act")
                self.cp(V[:, g * 4:(g + 1) * 4, 128:192], bv[:, :, 64:128], eng="act")
            else:
                self.cp(V[:, g * 4:(g + 1) * 4, :], bv, eng="act")
            yield
        n = 0
        for W, DST in ((WK, KT), (WQ, QT)):
            for tt_ in range(4 if "q" not in os.environ.get("KSKIP", "") else 0):
                cols = slice(tt_ * 512, (tt_ + 1) * 512)
                acc = self.banks[4] if layer == 0 else self.banks[6]
                prm = self.banks[5] if layer == 0 else self.banks[7]
                raw = self.RAW[:, n % 2, :]
                n += 1
                for k in range(8):
                    self.mm(acc[:], W[:, k, :], self.XNT[:, k, cols], start=(k == 0), stop=(k == 7))
                self.cp(raw, acc[:], eng="act")
                self.mm(prm[:], self.PERM[:], raw)
                raw2 = self.next_pt()
                self.cp(raw2, prm[:], eng="act")
                t1 = self.TMP[:, 0, :]
                t2 = self.TMP[:, 1, :]
                self.tt(t1, raw, self.CT[:, cols], ALU.mult)
                self.tt(t2, raw2, self.ST[:, cols], ALU.mult)
                self.tt(DST[:, cols], t1, t2, ALU.add)
                yield

    def moba_gate(self, b):
        QT, KT, V = self.qkv_views(b, 0)
        self.red(self.KM[:], KT.rearrange("p (n t) -> p n t", n=8), ALU.add)
        self.ts(self.KMH[:], self.KM[:], 1.0 / 256, None, ALU.mult)
        self.cp(self.T8[:, 24:32], self.KMH[:])
        self.stt(self.KML[:], self.KM[:], 1.0 / 256, self.T8[:, 24:32], ALU.mult, ALU.subtract)
        self.memset(self.GM[:], -1e30)
        self.memset(self.SELB[:], 0.0)
        G = self.banks[7]
        TB = self.banks[6][:].bitcast(BF16)
        for c in range(8, 16):
            qb = c // 2
            for hh in range(2):
                ps = slice(hh * 64, (hh + 1) * 64)
                Gh = (G, self.banks[5])[hh]
                self.mm(Gh[:, 0:8], QT[ps, c * 128:(c + 1) * 128], self.KMH[ps, :], start=True, stop=False)
                self.mm(Gh[:, 0:8], QT[ps, c * 128:(c + 1) * 128], self.KML[ps, :], start=False, stop=True)
            for hh in range(2):
                Gh = (G, self.banks[5])[hh]
                self.cp(self.GM[:, hh * 8:hh * 8 + qb], Gh[:, 0:qb])
            for hh in range(2):
                self.max8(self.T8[:, hh * 8:(hh + 1) * 8], self.GM[:, hh * 8:(hh + 1) * 8])
            for hh in range(2):
                self.tt(self.T8[:, hh * 8 + 7:hh * 8 + 8], self.T8[:, hh * 8 + 2:hh * 8 + 3], self.T8[:, hh * 8 + 3:hh * 8 + 4], ALU.add)
                self.stt(self.T8[:, 16 + hh * 8:16 + hh * 8 + qb], self.GM[:, hh * 8:hh * 8 + qb], 2.0,
                         self.T8[:, hh * 8 + 7:hh * 8 + 8].to_broadcast([128, qb]), ALU.mult, ALU.is_lt)
                self.ts(self.SELB[:, hh * 64:hh * 64 + qb], self.T8[:, 16 + hh * 8:16 + hh * 8 + qb], -BIG, None, ALU.mult)
            self.tr(TB[:, 0:128], self.SELB[:], self.IDENT[:])
            self.cp(self.SELBT[:, c * 128:(c + 1) * 128], TB[:, 0:128], eng="act")
            yield

    def attn0(self, b, mixer, ot_chunk):
        QT, KT, V = self.qkv_views(b, 0)
        units = []
        for qt in range(4):
            nk = 4 * qt + 4
            for kc in range(nk):
                units.append((qt, kc, nk))
        sbanks = ((self.banks[0], self.banks[1]), (self.banks[6], self.banks[7]))

        def emit_s(u):
            qt, kc, nk = units[u]
            col0 = max(0, kc * 128 - qt * 512)
            qs = slice(qt * 512 + col0, (qt + 1) * 512)
            cs = slice(col0, 512)
            need_bias = (mixer == "B" and qt >= 2)
            for hh in range(2):
                ps = slice(hh * 64, (hh + 1) * 64)
                Sb = sbanks[u % 2][hh]
                self.mm(Sb[:, cs], KT[ps, kc * 128:(kc + 1) * 128], QT[ps, qs], start=True, stop=not need_bias)
            if need_bias:
                for hh in range(2):
                    ps = slice(hh * 64, (hh + 1) * 64)
                    Sb = sbanks[u % 2][hh]
                    r = hh * 64 + kc // 2
                    oh = self.IDENT[ps, r:r + 1].to_broadcast([64, 128])
                    self.mm(Sb[:, cs], oh, self.SELBT[ps, qs], start=False, stop=True)

        def emit_rest(u):
            qt, kc, nk = units[u]
            col0 = max(0, kc * 128 - qt * 512)
            cs = slice(col0, 512)
            for hh in range(2):
                Sb = sbanks[u % 2][hh]
                O = self.banks[2 + hh]
                pt = self.next_pt()
                self.act(pt[:, cs], Sb[:, cs], AF.Exp, scale=0.125)
                if mixer == "A":
                    off = 128 * (4 * qt - kc) + 384
                    self.tt(pt[:, cs], pt[:, cs], self.TM[:, off + col0:off + 512], ALU.mult)
                elif kc >= 4 * qt:
                    self.tt(pt[:, col0:col0 + 128], pt[:, col0:col0 + 128], self.TRI[:], ALU.mult)
                lhs = V[:, kc, 0:128] if hh == 0 else V[:, kc, 64:192]
                self.mm(O[:, cs], lhs, pt[:, cs], start=(kc == 0), stop=(kc == nk - 1))
            if kc == nk - 1:
                qcols = slice(qt * 512, (qt + 1) * 512)
                rc = self.TMP[:, 0, :]
                O0 = self.banks[2]
                O1 = self.banks[3]
                self.recip(rc[0:64, :], O0[64:128, :], fast=True)
                self.tt(self.OT[0:64, ot_chunk, qcols], rc[0:64, :], O0[0:64, :], ALU.mult)
                self.recip(rc[64:128, :], O1[0:64, :], fast=True)
                self.tt(self.OT[64:128, ot_chunk, qcols], rc[64:128, :], O1[64:128, :], ALU.mult)
                return True
            return False

        n = len(units)
        for u in range(n + 1):
            if u < n:
                emit_s(u)
            if u >= 1:
                if emit_rest(u - 1):
                    yield
                    yield

    def attn1(self, b, h):
        QT, KT, V = self.qkv_views(b, 1)
        OB = (self.banks[2], self.banks[4])
        DB = (self.banks[3], self.banks[5])
        tiles = []
        for qt in range(4):
            nk = 4 * qt + 4
            for i in range(2):
                for kc in range(nk):
                    tiles.append((qt, i, kc, nk))

        def emit_s(idx):
            qt, i, kc, nk = tiles[idx]
            ps = slice(i * 64, (i + 1) * 64)
            Sb = self.banks[idx % 2]
            col0 = max(0, kc * 128 - qt * 512)
            qs = slice(qt * 512 + col0, (qt + 1) * 512)
            cs = slice(col0, 512)
            self.mm(Sb[:, cs], KT[ps, kc * 128:(kc + 1) * 128], QT[ps, qs])

        def emit_rest(idx):
            qt, i, kc, nk = tiles[idx]
            Sb = self.banks[idx % 2]
            col0 = max(0, kc * 128 - qt * 512)
            cs = slice(col0, 512)
            pt = self.next_pt()
            self.act(pt[:, cs], Sb[:, cs], AF.Exp, scale=0.125)
            if kc >= 4 * qt:
                self.tt(pt[:, col0:col0 + 128], pt[:, col0:col0 + 128], self.TRI[:], ALU.mult)
            self.mm(OB[i][:, cs], V[:, kc, :], pt[:, cs], start=(kc == 0), stop=(kc == nk - 1))
            self.mm(DB[i][:, cs], self.ONES[:], pt[:, cs], start=(kc == 0), stop=(kc == nk - 1))
            if kc == nk - 1 and i == 1:
                qcols = slice(qt * 512, (qt + 1) * 512)
                t1 = self.TMP[:, 0, :]
                t2 = self.TMP[:, 1, :]
                self.recip(t1, DB[0][:], fast=True)
                self.tt(t1, t1, OB[0][:], ALU.mult)
                self.recip(t2, DB[1][:], fast=True)
                self.tt(t2, t2, OB[1][:], ALU.mult)
                self.stt(t1, t2, self.NEGLAM, t1, ALU.mult, ALU.add)
                osq = self.next_pt()
                self.act(osq, t1, AF.Square)
                SSb = self.banks[7]
                self.mm(SSb[:], self.ONES[:], osq)
                self.ts(t2, SSb[:], 1.0 / 128, EPS, ALU.mult, ALU.add)
                self.act(t2, t2, AF.Ln)
                self.act(t2, t2, AF.Exp, scale=-0.5)
                self.stt(self.OT[:, h, qcols], t1, self.GS[:, 0:1], t2, ALU.mult, ALU.mult)
                return True
            return kc == nk - 1

        n = len(tiles)
        for idx in range(n + 1):
            if idx < n:
                emit_s(idx)
            if idx >= 1:
                if emit_rest(idx - 1):
                    yield

    def outproj(self, w_out):
        wv = w_out[0]
        slots = (6, 0)
        for nh in range(2):
            s0 = slots[nh]
            WO = self.WB[:, s0:s0 + 4, :].rearrange("p s n -> p (s n)").rearrange("p (k n) -> p k n", k=8)
            self.dma(WO, wv[:, nh * 512:(nh + 1) * 512].rearrange("(k p) n -> p k n", p=128))
            for c in range(16):
                bank = self.banks[4 + (c % 4)]
                for k in range(8):
                    self.mm(bank[:], self.OT[:, k, c * 128:(c + 1) * 128], WO[:, k, :], start=(k == 0), stop=(k == 7))
                hs = self.H[:, c, nh * 512:(nh + 1) * 512]
                self.tt(hs, hs, bank[:], ALU.add)
                yield

    def ffn(self, l):
        wg = self.w_gate[l]
        wu = self.w_up[l]
        wd = self.w_down[l]
        HID = self.HID
        for th in range(2):
            t0 = th * 1024
            pend = []

            def load_gu(j):
                s0 = (2 * j) % NSLOT
                self.dma(self.wslot(s0), wg[:, j * 128:(j + 1) * 128].rearrange("(k p) n -> p k n", p=128))
                self.dma(self.wslot(s0 + 1), wu[:, j * 128:(j + 1) * 128].rearrange("(k p) n -> p k n", p=128))
            PRE = 4
            for j in range(min(PRE, NJ)):
                load_gu(j)
            n = 0
            for j in range(NJ):
                s0 = (2 * j) % NSLOT
                WG = self.wslot(s0)
                WU = self.wslot(s0 + 1)
                for t2 in range(2):
                    cols = slice(t0 + t2 * 512, t0 + (t2 + 1) * 512)
                    Bg = self.banks[n % 2]
                    Bu = self.banks[2 + (n % 2)]
                    n += 1
                    for k in range(8):
                        self.mm(Bg[:], WG[:, k, :], self.XNT[:, k, cols], start=(k == 0), stop=(k == 7))
                    for k in range(8):
                        self.mm(Bu[:], WU[:, k, :], self.XNT[:, k, cols], start=(k == 0), stop=(k == 7))
                    sg = self.TMP[:, n % 2, :]
                    self.act(sg, Bg[:], AF.Silu)
                    self.tt(HID[:, j, t2 * 512:(t2 + 1) * 512], sg, Bu[:], ALU.mult)
                if j + PRE < NJ:
                    load_gu(j + PRE)
                yield
            for nh in range(2):
                def load_d(j, nh=nh):
                    self.dma(self.WB[:, j % NSLOT, 0:512], wd[j * 128:(j + 1) * 128, nh * 512:(nh + 1) * 512])
                PD = NSLOT - 1
                for j in range(PD):
                    load_d(j)
                for j in range(NJ):
                    WD = self.WB[:, j % NSLOT, 0:512]
                    for cc in range(8):
                        self.mm(self.banks[cc][:], HID[:, j, cc * 128:(cc + 1) * 128], WD, start=(j == 0), stop=(j == NJ - 1))
                    if j + PD < NJ:
                        load_d(j + PD)
                    if j % 4 == 3:
                        yield
                for cc in range(8):
                    c = t0 // 128 + cc
                    hs = self.H[:, c, nh * 512:(nh + 1) * 512]
                    self.tt(hs, hs, self.banks[cc][:], ALU.add)
                yield

    def run(self, gen):
        for _ in gen:
            pass

    def run_interleaved(self, main, side, ratio=2):
        side_done = side is None
        for _ in main:
            if not side_done:
                for _r in range(ratio):
                    try:
                        next(side)
                    except StopIteration:
                        side_done = True
                        break
        if not side_done:
            for _ in side:
                pass

    def layer0_mixer_v2(self):
        self.run(self.norm_to_xnt(self.ab_norm_g[0:1, :]))
        chunks = []
        for j in range(4):
            chunks.append(("A", j * 128, 512 + j * 128, 1024 + j * 128, j))
        for j in range(4):
            chunks.append(("B", 1536 + j * 128, 2048 + j * 128, 2560 + j * 128, 4 + j))
        n = len(chunks)
        for i in range(2):
            self.load_inproj_w(self.ab_w_in, chunks[i][1], chunks[i][2], chunks[i][3], i)
        self.dma(self.TM, self.c_tm, q="sp")
        self.run(self.inproj(0, 0))
        for i, (mixer, cq, ck, cv, otc) in enumerate(chunks):
            b = i % 2
            if mixer == "B":
                self.run(self.moba_gate(b))
            side = self.inproj(1 - b, 0) if i + 1 < n else None
            self.run_interleaved(self.attn0(b, mixer, otc), side, ratio=2)
            if i + 2 < n:
                nx = chunks[i + 2]
                self.load_inproj_w(self.ab_w_in, nx[1], nx[2], nx[3], b)
        self.run(self.outproj(self.ab_w_out))

    def layer1_mixer_v2(self):
        self.run(self.norm_to_xnt(self.diff_norm_g[0:1, :]))
        for h in range(2):
            self.load_inproj_w(self.diff_w_in, h * 128, 1024 + h * 128, 2048 + h * 128, h)
        self.run(self.inproj(0, 1))
        for h in range(8):
            b = h % 2
            side = self.inproj(1 - b, 1) if h + 1 < 8 else None
            self.run_interleaved(self.attn1(b, h), side, ratio=1)
            if h + 2 < 8:
                self.load_inproj_w(self.diff_w_in, (h + 2) * 128, 1024 + (h + 2) * 128, 2048 + (h + 2) * 128, b)
        self.run(self.outproj(self.diff_w_out))

    def layer0_mixer(self):
        self.run(self.norm_to_xnt(self.ab_norm_g[0:1, :]))
        chunks = []
        for j in range(4):
            chunks.append(("A", j * 128, 512 + j * 128, 1024 + j * 128, j))
        for j in range(4):
            chunks.append(("B", 1536 + j * 128, 2048 + j * 128, 2560 + j * 128, 4 + j))
        self.load_inproj_w(self.ab_w_in, chunks[0][1], chunks[0][2], chunks[0][3], 0)
        cur_scr = None
        kdbg = int(os.environ.get("KDBG", "99"))
        for i, (mixer, cq, ck, cv, otc) in enumerate(chunks):
            b = i % 2
            if i > kdbg:
                break
            if i + 1 < len(chunks):
                nx = chunks[i + 1]
                self.load_inproj_w(self.ab_w_in, nx[1], nx[2], nx[3], 1 - b)
            if "i" not in os.environ.get("KSKIP", ""):
                self.run(self.inproj(b, 0))
            if i == kdbg:
                break
            if mixer == "A" and cur_scr != "TM":
                self.dma(self.TM, self.c_tm, q="sp")
                cur_scr = "TM"
            if mixer == "B":
                cur_scr = "SELB"
                self.run(self.moba_gate(b))
                if os.environ.get("KDUMP") and i == 4:
                    yv = self.y[0]
                    self.dbg_outs = [self.dma(yv[0:128, 0:16], self.GM[:], q="sp", sb_out=False, sb_in=True),
                                     self.dma(yv[0:128, 16:48], self.T8[:], q="sp", sb_out=False, sb_in=True),
                                     self.dma(yv[0:128, 48:56], self.KM[:], q="sp", sb_out=False, sb_in=True)]
                    return
            self.run(self.attn0(b, mixer, otc))
        if "o" not in os.environ.get("KSKIP", ""):
            self.run(self.outproj(self.ab_w_out))

    def layer1_mixer(self):
        self.run(self.norm_to_xnt(self.diff_norm_g[0:1, :]))
        self.load_inproj_w(self.diff_w_in, 0, 1024, 2048, 0)
        for h in range(8):
            b = h % 2
            if h + 1 < 8:
                self.load_inproj_w(self.diff_w_in, (h + 1) * 128, 1024 + (h + 1) * 128, 2048 + (h + 1) * 128, 1 - b)
            self.run(self.inproj(b, 1))
            self.run(self.attn1(b, h))
        self.run(self.outproj(self.diff_w_out))

    def build(self):
        self.load_consts()
        outs = []
        stages = ["attn0", "l0", "attn1", "l1", None]
        stop = self.stop_after
        for s_ in range(2):
            self.dma(self.H[:], self.x[s_].rearrange("(c p) d -> p c d", p=128), q="sp")
            if stop == "x":
                outs += self.final_norm(s_, raw=True)
                continue
            self.rope_tables(s_)
            if stop == "rope":
                outs += self.final_norm(s_, raw=True)
                continue
            if stop == "norm0":
                self.run(self.norm_to_xnt(self.ab_norm_g[0:1, :]))
                outs += self.final_norm(s_, raw=True)
                continue
            done = False
            for stage in stages:
                if stage == "attn0":
                    if os.environ.get("KOLD"):
                        self.layer0_mixer()
                    else:
                        self.layer0_mixer_v2()
                    if os.environ.get("KDUMP"):
                        self.R.add("sp", None, extra_deps=self.dbg_outs)
                        return
                elif stage == "l0":
                    self.run(self.norm_to_xnt(self.ffn_norm_g[0:1, :]))
                    self.run(self.ffn(0))
                elif stage == "attn1":
                    if os.environ.get("KOLD"):
                        self.layer1_mixer()
                    else:
                        self.layer1_mixer_v2()
                elif stage == "l1":
                    self.run(self.norm_to_xnt(self.ffn_norm_g[1:2, :]))
                    self.run(self.ffn(1))
                if stage == stop:
                    outs += self.final_norm(s_, raw=(stage is not None))
                    done = True
                    break
            assert done
        self.R.add("sp", None, extra_deps=outs)


def build_nc(stop_after=None):
    _reg_cache.clear()
    nc = bass.Bass("TRN2", target_bir_lowering=False)
    with ExitStack() as stack:
        P = Prog(nc, stack, stop_after=stop_after)
        P.build()
        P.R.finalize(nc, stack)
        block = stack.enter_context(nc.Block())

        @block.tensor
        def _(e):
            P.R.emit("pe", e)

        @block.scalar
        def _(e):
            P.R.emit("act", e)

        @block.vector
        def _(e):
            P.R.emit("dve", e)

        @block.gpsimd
        def _(e):
            P.R.emit("pool", e)

        @block.sync
        def _(e):
            P.R.emit("sp", e)
    return nc


def make_consts():
    bf = ml_dtypes.bfloat16
    ident = np.eye(128, dtype=np.float32).astype(bf)
    perm = np.zeros((128, 128), dtype=np.float32)
    invf = np.zeros((128, 1), dtype=np.float32)
    inv_freq = (np.float32(500000.0) ** (-np.arange(0, 16, 2, dtype=np.float32) / np.float32(16))).astype(np.float32)
    for d_ in range(128):
        dd = d_ % 64
        if dd < 8:
            perm[d_ + 8, d_] = 1.0
            invf[d_, 0] = -inv_freq[dd]
        elif dd < 16:
            perm[d_ - 8, d_] = 1.0
            invf[d_, 0] = inv_freq[dd - 8]
    kk = np.arange(128)[:, None]
    qq = np.arange(128)[None, :]
    tri = (qq >= kk).astype(np.float32).astype(bf)
    jj = np.arange(TMW)[None, :]
    dist = jj - kk - 384
    cnt = ((dist >= 0) & (dist <= 128)).astype(np.float32)
    cnt += ((dist >= 0) & (dist % 4 == 0) & (dist <= 512)).astype(np.float32)
    cnt += ((dist >= 0) & (dist % 16 == 0) & (dist <= 2048)).astype(np.float32)
    return {"c_ident": ident, "c_perm": perm.astype(bf), "c_tri": tri, "c_tm": cnt.astype(bf), "c_invf": invf}


_NC_CACHE = {}


def kernel(stop_after=None, **inputs):
    key = stop_after
    if key not in _NC_CACHE:
        _NC_CACHE[key] = build_nc(stop_after)
    nc = _NC_CACHE[key]
    consts = make_consts()
    x = np.ascontiguousarray(inputs["x"], dtype=np.float32)
    pos = np.ascontiguousarray(inputs["positions"], dtype=np.int32)
    shared = {k: np.ascontiguousarray(v) for k, v in inputs.items() if k not in ("x", "positions")}
    shared["final_norm_g"] = shared["final_norm_g"].reshape(1, D)
    shared.update(consts)
    in_maps = []
    ncores = int(os.environ.get("KCORES", "8"))
    for c in range(ncores):
        m = dict(shared)
        m["x"] = x[2 * c:2 * c + 2]
        m["positions"] = pos[2 * c:2 * c + 2]
        in_maps.append(m)
    res = run_bass_kernel_spmd(nc, in_maps, core_ids=list(range(ncores)))
    return np.concatenate([np.asarray(r["y"]) for r in res.results], axis=0).astype(np.float32)
```
